# Optimizing a Trainium2 kernel written in Bass

```python
import math
import jax, jax.numpy as jnp
from jax import lax
import numpy as np

D_MODEL = 1024
BATCH = 16
SEQ = 2048
DEPTH = 4
DEC_BATCH = 8
DEC_SEQ = 8192
PAST_LEN = 128

HEAD_DIM = 64
N_HEADS_TOTAL = D_MODEL // HEAD_DIM
N_HEADS_B = N_HEADS_TOTAL // 4
N_HEADS_A = (N_HEADS_TOTAL - N_HEADS_B) // 2
N_HEADS_C = N_HEADS_TOTAL - N_HEADS_A - N_HEADS_B
W_A = N_HEADS_A * HEAD_DIM
W_B = N_HEADS_B * HEAD_DIM
W_C = N_HEADS_C * HEAD_DIM
MIX_WIDTH = W_A + W_B + W_C
DILATED_BRANCHES = ((128, 1), (512, 4), (2048, 16))
DIFF_DIM = HEAD_DIM // 2
ROPE_THETA = 500000.0
ROPE_DIM_A = HEAD_DIM // 4
ROPE_DIM_B = DIFF_DIM // 4
Q_BLOCK = 128
LORA_W = 64
LORA_A = 64
LORA_V = 32
LORA_G = 128
C_SPLITS = [W_C, 2 * W_C, 3 * W_C, 3 * W_C + LORA_W, 3 * W_C + 2 * LORA_W,
            3 * W_C + 2 * LORA_W + LORA_A, 3 * W_C + 2 * LORA_W + 2 * LORA_A]
C_WIDTH = 3 * W_C + 2 * LORA_W + 2 * LORA_A + LORA_G
IN_WIDTH = 3 * W_A + 3 * W_B + C_WIDTH
D_FF = ((8 * D_MODEL + 2) // 3 + 255) // 256 * 256
RMS_EPS = 1e-6
SUBLN_EPS = 1e-5
LNX_EPS = 64e-5
NEG_INF = -1e30

kernel_name = 'hybrid_dilated_diff_rwkv7_encoder'


def rms_norm(x, g, eps):
    xf = x.astype(jnp.float32)
    y = xf * lax.rsqrt(jnp.mean(xf * xf, axis=-1, keepdims=True) + eps)
    return (y * g.astype(jnp.float32)).astype(x.dtype)


def partial_rope(x, rot_dim):
    s = x.shape[-2]
    half = rot_dim // 2
    inv = ROPE_THETA ** (-(jnp.arange(half, dtype=jnp.float32) * 2.0 / rot_dim))
    ang = jnp.arange(s, dtype=jnp.float32)[:, None] * inv[None, :]
    cos, sin = jnp.cos(ang), jnp.sin(ang)
    xf = x.astype(jnp.float32)
    x1, x2, rest = xf[..., :half], xf[..., half:rot_dim], xf[..., rot_dim:]
    out = jnp.concatenate([x1 * cos - x2 * sin, x2 * cos + x1 * sin, rest], axis=-1)
    return out.astype(x.dtype)


def split_heads(t, n_heads):
    b, s, _ = t.shape
    return t.reshape(b, s, n_heads, -1).transpose(0, 2, 1, 3)


def merge_heads(t):
    b, h, s, d = t.shape
    return t.transpose(0, 2, 1, 3).reshape(b, s, h * d)


def dilated_branch(q, k, v, window, dil):
    b, h, s, dh = q.shape
    half = window // (2 * dil)
    L = s // dil
    nb = -(-L // half)
    Lp = nb * half

    def split(t):
        return t.reshape(b, h, L, dil, dh).swapaxes(2, 3)

    qs = jnp.pad(split(q), ((0, 0), (0, 0), (0, 0), (0, Lp - L), (0, 0))).reshape(b, h, dil, nb, half, dh)

    def key_blocks(t):
        tp = jnp.pad(split(t), ((0, 0), (0, 0), (0, 0), (half, Lp - L + half), (0, 0)))
        tp = tp.reshape(b, h, dil, nb + 2, half, dh)
        return jnp.concatenate([tp[:, :, :, :-2], tp[:, :, :, 1:-1], tp[:, :, :, 2:]], axis=4)

    kb, vb = key_blocks(k), key_blocks(v)
    qi = jnp.arange(nb)[:, None, None] * half + jnp.arange(half)[None, :, None]
    ki = (jnp.arange(nb)[:, None, None] - 1) * half + jnp.arange(3 * half)[None, None, :]
    valid = (jnp.abs(ki - qi) <= half) & (ki >= 0) & (ki < L)
    logits = jnp.einsum('bhrnqd,bhrnkd->bhrnqk', qs, kb,
                        preferred_element_type=jnp.float32) * (dh ** -0.5)
    logits = jnp.where(valid, logits, NEG_INF)
    lse = jax.nn.logsumexp(logits, axis=-1)
    p = jnp.exp(logits - lse[..., None]).astype(v.dtype)
    o = jnp.einsum('bhrnqk,bhrnkd->bhrnqd', p, vb)

    def merge(t):
        trail = t.shape[5:]
        t = t.reshape((b, h, dil, Lp) + trail)[:, :, :, :L]
        return t.swapaxes(2, 3).reshape((b, h, s) + trail)

    return merge(o), merge(lse)


def dilated_mixture(q, k, v):
    outs, lses = [], []
    for window, dil in DILATED_BRANCHES:
        o, lse = dilated_branch(q, k, v, window, dil)
        outs.append(o)
        lses.append(lse)
    wts = jax.nn.softmax(jnp.stack(lses, axis=0), axis=0)
    out = jnp.einsum('nbhs,nbhsd->bhsd', wts, jnp.stack(outs, axis=0).astype(jnp.float32))
    return out.astype(q.dtype)


def diff_attention(q, k, v, lam, lam_init, subln_w):
    b, h, _, s, dm = q.shape
    nq = s // Q_BLOCK
    qb = q.reshape(b, h, 2, nq, Q_BLOCK, dm).transpose(3, 0, 1, 2, 4, 5)
    scale = dm ** -0.5

    def block(q_blk):
        sc = jnp.einsum('bhmqd,bhmkd->bhmqk', q_blk, k,
                        preferred_element_type=jnp.float32) * scale
        p = jax.nn.softmax(sc, axis=-1)
        a = p[:, :, 0] - lam * p[:, :, 1]
        return jnp.einsum('bhqk,bhkd->bhqd', a.astype(v.dtype), v)

    o = lax.map(block, qb)
    o = o.transpose(1, 2, 0, 3, 4).reshape(b, h, s, -1)
    o = rms_norm(o, subln_w, SUBLN_EPS)
    return o * (1.0 - lam_init)


def token_shift_centred(z, mu):
    zp = jnp.pad(z, ((0, 0), (1, 1), (0, 0)))
    nbr = 0.5 * (zp[:, :-2] + zp[:, 2:])
    return z + (nbr - z) * mu


def rwkv7_bidirectional(r, k, v, wl, al, gl, w0, w_up, a0, a_up, g_up, k_k, k_a, r_k, lnx_w, lnx_b):
    b, s, c = r.shape
    H, N = N_HEADS_C, HEAD_DIM
    f32 = jnp.float32
    wlr = jnp.einsum('dbsr,drc->dbsc', jnp.tanh(wl.astype(f32)), w_up.astype(f32)) \
        + w0.astype(f32)[:, None, None, :]
    w = -jax.nn.softplus(-wlr) - 0.5
    decay = jnp.exp(-jnp.exp(w))
    a = jax.nn.sigmoid(jnp.einsum('dbsr,drc->dbsc', al.astype(f32), a_up.astype(f32))
                       + a0.astype(f32)[:, None, None, :])
    g = jax.nn.sigmoid(gl.astype(f32)) @ g_up.astype(f32)
    rf, kf, vf = r.astype(f32), k.astype(f32), v.astype(f32)
    kk = (kf * k_k.astype(f32)).reshape(b, s, H, N)
    kk = kk / jnp.maximum(jnp.sqrt(jnp.sum(kk * kk, axis=-1, keepdims=True)), 1e-12)
    kk = kk.reshape(b, s, c)
    kd = kf[None] * (1.0 + (a - 1.0) * k_a.astype(f32))

    def both(t):
        return jnp.stack([t, jnp.flip(t, 1)])

    def orient(t):
        return jnp.stack([t[0], jnp.flip(t[1], 1)])

    def to_scan(t):
        return t.reshape(2, b, s, H, N).transpose(2, 0, 1, 3, 4)

    xs = (to_scan(both(rf)), to_scan(orient(decay)), to_scan(orient(kd)),
          to_scan(both(vf)), to_scan(both(-kk)), to_scan(orient(kk[None] * a)))

    def step(state, inp):
        r_t, w_t, k_t, v_t, a_t, b_t = inp
        sa = jnp.einsum('dbhvk,dbhk->dbhv', state, a_t)
        state = state * w_t[..., None, :] + sa[..., :, None] * b_t[..., None, :] \
            + v_t[..., :, None] * k_t[..., None, :]
        y = jnp.einsum('dbhvk,dbhk->dbhv', state, r_t)
        return state, y

    state0 = jnp.zeros((2, b, H, N, N), f32)
    _, ys = lax.scan(step, state0, xs)
    ys = ys.transpose(1, 2, 0, 3, 4)
    y = ys[0] + jnp.flip(ys[1], 1)
    mu = jnp.mean(y, axis=-1, keepdims=True)
    var = jnp.mean(jnp.square(y - mu), axis=-1, keepdims=True)
    y = ((y - mu) * lax.rsqrt(var + LNX_EPS)).reshape(b, s, c) * lnx_w.astype(f32) + lnx_b.astype(f32)
    bonus = jnp.sum(rf.reshape(b, s, H, N)[None] * kd.reshape(2, b, s, H, N) * r_k.astype(f32),
                    axis=(0, -1))[..., None] * vf.reshape(b, s, H, N)
    out = (y + bonus.reshape(b, s, c)) * g
    return out.astype(r.dtype)


def encoder_trunk(x, norm_mix, w_in, w_out, lam_qk, subln_w, mu_c, w0, w_up, a0, a_up, g_up,
                  v0, v_down, v_up, k_k, k_a, r_k, lnx_w, lnx_b, norm_ffn, w_gate_up, w_down,
                  norm_final):
    b, s, _ = x.shape
    v_first = None
    for l in range(DEPTH):
        h = rms_norm(x, norm_mix[l], RMS_EPS)
        z = h @ w_in[l]
        z_a, z_b, z_c = jnp.split(z, [3 * W_A, 3 * W_A + 3 * W_B], axis=-1)

        qa, ka, va = (split_heads(t, N_HEADS_A) for t in jnp.split(z_a, 3, axis=-1))
        qa = partial_rope(qa, ROPE_DIM_A)
        ka = partial_rope(ka, ROPE_DIM_A)
        o_a = merge_heads(dilated_mixture(qa, ka, va))

        qb, kb, vb = (split_heads(t, N_HEADS_B) for t in jnp.split(z_b, 3, axis=-1))
        to_maps = lambda t: t.reshape(b, N_HEADS_B, s, 2, DIFF_DIM).transpose(0, 1, 3, 2, 4)
        qb = partial_rope(to_maps(qb), ROPE_DIM_B)
        kb = partial_rope(to_maps(kb), ROPE_DIM_B)
        lp = lam_qk[l].astype(jnp.float32)
        lam_init = 0.8 - 0.6 * math.exp(-0.3 * l)
        lam = jnp.exp(jnp.sum(lp[0] * lp[1])) - jnp.exp(jnp.sum(lp[2] * lp[3])) + lam_init
        o_b = merge_heads(diff_attention(qb, kb, vb, lam, lam_init, subln_w[l]))

        zc = token_shift_centred(z_c, mu_c[l])
        r, kc, vc, wlf, wlb, alf, alb, gl = jnp.split(zc, C_SPLITS, axis=-1)
        if l == 0:
            v_first = vc
        else:
            vg = jax.nn.sigmoid(v0[l - 1] + (vc @ v_down[l - 1]) @ v_up[l - 1])
            vc = vc + (v_first - vc) * vg
        o_c = rwkv7_bidirectional(r, kc, vc, jnp.stack([wlf, wlb]), jnp.stack([alf, alb]), gl,
                                  w0[l], w_up[l], a0[l], a_up[l], g_up[l], k_k[l], k_a[l],
                                  r_k[l], lnx_w[l], lnx_b[l])

        x = x + jnp.concatenate([o_a, o_b, o_c], axis=-1) @ w_out[l]

        h = rms_norm(x, norm_ffn[l], RMS_EPS)
        gate, up = jnp.split(h @ w_gate_up[l], 2, axis=-1)
        x = x + (jax.nn.silu(gate) * up) @ w_down[l]
    return rms_norm(x, norm_final, RMS_EPS)


def setup_inputs(seed: int = 0) -> dict:
    key = jax.random.key(seed)
    ks = jax.random.split(key, 25)
    f32 = jnp.float32

    def nrm(i, shape, scale):
        return scale * jax.random.normal(ks[i], shape, f32)

    def unif(i, shape, lo, hi):
        return jax.random.uniform(ks[i], shape, f32, lo, hi)

    return {
        'x_prompt': nrm(0, (BATCH, SEQ, D_MODEL), 1.0),
        'x_sample': nrm(1, (DEC_BATCH, DEC_SEQ, D_MODEL), 1.0),
        'norm_mix': 1.0 + nrm(2, (DEPTH, D_MODEL), 0.02),
        'w_in': nrm(3, (DEPTH, D_MODEL, IN_WIDTH), D_MODEL ** -0.5),
        'w_out': nrm(4, (DEPTH, MIX_WIDTH, D_MODEL), MIX_WIDTH ** -0.5),
        'lam_qk': nrm(5, (DEPTH, 4, DIFF_DIM), 0.1),
        'subln_w': 1.0 + nrm(6, (DEPTH, HEAD_DIM), 0.02),
        'mu_c': unif(7, (DEPTH, C_WIDTH), 0.0, 1.0),
        'w0': unif(8, (DEPTH, 2, W_C), -6.0, 1.0),
        'w_up': nrm(9, (DEPTH, 2, LORA_W, W_C), 0.5 * LORA_W ** -0.5),
        'a0': nrm(10, (DEPTH, 2, W_C), 0.5),
        'a_up': nrm(11, (DEPTH, 2, LORA_A, W_C), 0.5 * LORA_A ** -0.5),
        'g_up': nrm(12, (DEPTH, LORA_G, W_C), LORA_G ** -0.5),
        'v0': 1.0 + nrm(13, (DEPTH - 1, W_C), 0.1),
        'v_down': nrm(14, (DEPTH - 1, W_C, LORA_V), W_C ** -0.5),
        'v_up': nrm(15, (DEPTH - 1, LORA_V, W_C), 0.5 * LORA_V ** -0.5),
        'k_k': 0.85 + nrm(16, (DEPTH, W_C), 0.02),
        'k_a': 1.0 + nrm(17, (DEPTH, W_C), 0.02),
        'r_k': nrm(18, (DEPTH, N_HEADS_C, HEAD_DIM), 0.1),
        'lnx_w': 1.0 + nrm(19, (DEPTH, W_C), 0.02),
        'lnx_b': nrm(20, (DEPTH, W_C), 0.02),
        'norm_ffn': 1.0 + nrm(21, (DEPTH, D_MODEL), 0.02),
        'w_gate_up': nrm(22, (DEPTH, D_MODEL, 2 * D_FF), D_MODEL ** -0.5),
        'w_down': nrm(23, (DEPTH, D_FF, D_MODEL), D_FF ** -0.5),
        'norm_final': 1.0 + nrm(24, (D_MODEL,), 0.02),
    }


def reference(x_prompt, x_sample, norm_mix, w_in, w_out, lam_qk, subln_w, mu_c, w0, w_up, a0,
              a_up, g_up, v0, v_down, v_up, k_k, k_a, r_k, lnx_w, lnx_b, norm_ffn, w_gate_up,
              w_down, norm_final):
    y_prompt = encoder_trunk(x_prompt, norm_mix, w_in, w_out, lam_qk, subln_w, mu_c, w0, w_up,
                             a0, a_up, g_up, v0, v_down, v_up, k_k, k_a, r_k, lnx_w, lnx_b,
                             norm_ffn, w_gate_up, w_down, norm_final)
    y_sample = encoder_trunk(x_sample, norm_mix, w_in, w_out, lam_qk, subln_w, mu_c, w0, w_up,
                             a0, a_up, g_up, v0, v_down, v_up, k_k, k_a, r_k, lnx_w, lnx_b,
                             norm_ffn, w_gate_up, w_down, norm_final)
    return (y_prompt, y_sample)
```

```python
import math
from contextlib import ExitStack
import numpy as np
import ml_dtypes
import concourse.bass as bass
import concourse.mybir as mybir
from concourse.bass_utils import run_bass_kernel_spmd

F32 = mybir.dt.float32
BF16 = mybir.dt.bfloat16
AF = mybir.ActivationFunctionType
ALU = mybir.AluOpType
AX = mybir.AxisListType

D = 1024
DEPTH = 4
HD = 64
NHA, NHB, NHC = 6, 4, 6
WA, WB, WC = 384, 256, 384
INW = 3456
DFF = 2816
CW = 1536
RMS_EPS = 1e-6
SUBLN_EPS = 1e-5
LNX_EPS = 64e-5
ROPE_THETA = 500000.0
NTAB = 416
SKIP_SAME_ENGINE = False


class Prog:
    def __init__(self, nc, st):
        self.nc = nc
        self.E = {'pe': nc.tensor, 'act': nc.scalar, 'dve': nc.vector, 'pool': nc.gpsimd, 'sp': nc.sync}
        self.sems = {}
        self.cnt = {}
        for e in ['pe', 'act', 'dve', 'pool']:
            k = ('e', e)
            self.sems[k] = st.enter_context(nc.semaphore('se_' + e))
            self.cnt[k] = 0
        self.ndq = 8
        for q in ['sp', 'pool']:
            for i in range(self.ndq):
                k = ('d', q, i)
                self.sems[k] = st.enter_context(nc.semaphore('sd_%s%d' % (q, i)))
                self.cnt[k] = 0
        self.dnext = {'sp': 0, 'pool': 0}
        self.waited = {}
        self.lw = {}
        self.rd = {}
        self.nops = 0

    def _wait(self, e, deps):
        for k, v in deps.items():
            if k == ('e', 'pe') and e == 'pe':
                continue
            if SKIP_SAME_ENGINE and k == ('e', e):
                continue
            if self.waited.get((e, k), 0) >= v:
                continue
            self.E[e].wait_ge(self.sems[k], v)
            self.waited[(e, k)] = v

    def _deps(self, reads, writes):
        d = {}
        for r in reads:
            t = self.lw.get(r)
            if t is not None and d.get(t[0], 0) < t[1]:
                d[t[0]] = t[1]
        for w in writes:
            t = self.lw.get(w)
            if t is not None and d.get(t[0], 0) < t[1]:
                d[t[0]] = t[1]
            for k, v in self.rd.get(w, {}).items():
                if d.get(k, 0) < v:
                    d[k] = v
        return d

    def _commit(self, tok, reads, writes):
        k, v = tok
        for r in reads:
            self.rd.setdefault(r, {})[k] = v
        for w in writes:
            self.lw[w] = tok
            self.rd[w] = {}

    max_ops = None

    @staticmethod
    def _psum_excl(reads, writes):
        pr = [r for r in reads if isinstance(r, str) and (r.startswith('pz') or r.startswith('pt'))]
        if pr:
            return list(reads), list(writes) + pr
        return reads, writes

    def op(self, e, fn, reads=(), writes=()):
        if self.max_ops is not None and self.nops >= self.max_ops:
            return
        reads, writes = self._psum_excl(reads, writes)
        self._wait(e, self._deps(reads, writes))
        ins = fn(self.E[e])
        k = ('e', e)
        self.cnt[k] += 1
        ins.then_inc(self.sems[k], 1)
        self._commit((k, self.cnt[k]), reads, writes)
        self.nops += 1

    def dma(self, q, out, in_, reads=(), writes=()):
        if self.max_ops is not None and self.nops >= self.max_ops:
            return
        i = self.dnext[q]
        self.dnext[q] = (i + 1) % self.ndq
        k = ('d', q, i)
        d = self._deps(reads, writes)
        if self.cnt[k] > 0 and d.get(k, 0) < self.cnt[k]:
            d[k] = self.cnt[k]
        self._wait(q, d)
        ins = self.E[q].dma_start(out=out, in_=in_)
        self.cnt[k] += 16
        ins.then_inc(self.sems[k], 16)
        self._commit((k, self.cnt[k]), reads, writes)
        self.nops += 1

    def barrier(self):
        for e in ['pe', 'act', 'dve', 'pool', 'sp']:
            d = {k: v for k, v in self.cnt.items() if v > 0}
            self._wait(e, d)
        self.lw = {}
        self.rd = {}


def build(seq_lens, depth, debug=False, stop_after=None):
    nc = bass.Bass("TRN2", target_bir_lowering=False)
    T = sum(seq_lens)
    NS = len(seq_lens)
    offs = [sum(seq_lens[:i]) for i in range(NS)]
    SMAX = max(seq_lens)

    def din(name, shape, dt=F32):
        return nc.dram_tensor(name, list(shape), dt, kind="ExternalInput").ap()

    def dscr(name, shape, dt=F32):
        return nc.dram_tensor(name, list(shape), dt, kind="ExternalOutput" if debug else "Internal").ap()

    x_in = din("x", [T, D])
    y_out = nc.dram_tensor("y", [T, D], F32, kind="ExternalOutput").ap()
    norm_mix = din("norm_mix", [DEPTH, D])
    w_in = din("w_in", [DEPTH, D, INW])
    w_out = din("w_out", [DEPTH, D, D])
    lam_qk = din("lam_qk", [DEPTH, 4 * 32])
    subln_w = din("subln_w", [DEPTH, 64])
    mu_c = din("mu_c", [DEPTH, CW])
    w0 = din("w0", [DEPTH, 2 * WC])
    w_up = din("w_up", [DEPTH, 2, 64, WC])
    a0 = din("a0", [DEPTH, 2 * WC])
    a_up = din("a_up", [DEPTH, 2, 64, WC])
    g_up = din("g_up", [DEPTH, 128, WC])
    v0 = din("v0", [DEPTH - 1, WC])
    v_down = din("v_down", [DEPTH - 1, WC, 32])
    v_up = din("v_up", [DEPTH - 1, 32, WC])
    k_k = din("k_k", [DEPTH, WC])
    k_a = din("k_a", [DEPTH, WC])
    r_k = din("r_k", [DEPTH, WC])
    lnx_w = din("lnx_w", [DEPTH, WC])
    lnx_b = din("lnx_b", [DEPTH, WC])
    norm_ffn = din("norm_ffn", [DEPTH, D])
    w_gate_up = din("w_gate_up", [DEPTH, D, 2 * DFF])
    w_down = din("w_down", [DEPTH, DFF, D])
    norm_final = din("norm_final", [1, D])
    rope_tab = din("rope_tab", [SMAX, NTAB])
    maskA = din("maskA", [128, 20 * 512], BF16)
    ident_f = din("ident_f", [128, 128])
    tri_c = din("tri_c", [128, 512])
    mask_c = din("mask_c", [128, 1792])

    XR = dscr("XR", [T, D])
    QKT = dscr("QKT", [1280, T], BF16)
    VA = dscr("VA", [T, NHA * 65], BF16)
    VB = dscr("VB", [T, NHB * 65], BF16)
    ZC = dscr("ZC", [T + 2 * NS, CW])
    OS = dscr("OS", [T, D], BF16)
    RW = dscr("RW", [T, 9 * WC])
    GB = dscr("GB", [T, 2 * WC])
    YF = dscr("YF", [T, WC])
    VF = dscr("VF", [T, WC])

    st = ExitStack()
    with st:
        P = Prog(nc, st)

        def sb(name, shape, dt=F32):
            return st.enter_context(nc.sbuf_tensor(name, list(shape), dt))

        def ps(name, shape, dt=F32):
            return st.enter_context(nc.psum_tensor(name, list(shape), dt))

        identf = sb("identf", [128, 128])
        identb = sb("identb", [128, 128], BF16)
        epsr = sb("epsr", [128, 1])
        zero_t = sb("zero_t", [128, 512])
        P.dma('sp', identf[:], ident_f[:, :], writes=['identf'])
        P.op('dve', lambda e: e.tensor_copy(out=identb[:], in_=identf[:]), reads=['identf'], writes=['identb'])
        P.op('dve', lambda e: e.memset(epsr[:], RMS_EPS), writes=['epsr'])
        P.op('dve', lambda e: e.memset(zero_t[:], 0.0), writes=['zero_t'])
        for s in range(NS):
            r0 = offs[s] + 2 * s
            r1 = r0 + seq_lens[s] + 1
            for cc in range(3):
                P.dma('sp', ZC[r0:r0 + 1, cc * 512:(cc + 1) * 512], zero_t[0:1, :], reads=['zero_t'], writes=[('ZCpad', s, 0, cc)])
                P.dma('sp', ZC[r1:r1 + 1, cc * 512:(cc + 1) * 512], zero_t[0:1, :], reads=['zero_t'], writes=[('ZCpad', s, 1, cc)])

        pzbig = [ps("pzb%d" % i, [128, 1024]) for i in range(2)]
        pz = [pzbig[0][:, 0:512], pzbig[0][:, 512:1024], pzbig[1][:, 0:512], pzbig[1][:, 512:1024],
              ps("pz4", [128, 512])[:], ps("pz5", [128, 512])[:]]
        pt = [ps("pt%d" % i, [128, 1024], BF16) for i in range(2)]
        cnt = {'pz': 0, 'pt': 0}

        def next_pz():
            i = cnt['pz'] % 4
            cnt['pz'] += 1
            return pz[i], 'pz%d' % i

        def next_pt():
            i = cnt['pt'] % 2
            cnt['pt'] += 1
            return pt[i], 'pt%d' % i

        stage = [sb("stage%d" % i, [128, 1024]) for i in range(2)]
        scnt = [0]

        def load_weight(dst, dname, src, KC, N):
            for kc in range(KC):
                c0 = 0
                while c0 < N:
                    w = min(1024, N - c0)
                    j = scnt[0] % 2
                    scnt[0] += 1
                    P.dma('sp', stage[j][:, 0:w], src[kc * 128:(kc + 1) * 128, c0:c0 + w], writes=['stage%d' % j])
                    P.op('pool', lambda e, j=j, w=w, kc=kc, c0=c0: e.tensor_copy(out=dst[:, kc, c0:c0 + w], in_=stage[j][:, 0:w]),
                         reads=['stage%d' % j], writes=[dname])
                    c0 += w

        def bcast_load(dst, dname, src_row):
            P.dma('sp', dst, src_row.partition_broadcast(128), writes=[dname])

        junk = sb("junk", [128, 1024], BF16)
        ssq = sb("ssq", [128, 1])
        rstd = sb("rstd", [128, 1])

        def rmsnorm_tile(x_t, xname, g_t, gname, out_t, oname, width=D, eps_t=None):
            P.op('dve', lambda e: e.scalar_tensor_tensor(out=junk[:, 0:width], in0=x_t, scalar=1.0, in1=x_t,
                                                         op0=ALU.mult, op1=ALU.mult, accum_out=ssq[:]),
                 reads=[xname], writes=['junk', 'ssq'])
            P.op('act', lambda e: e.activation(out=rstd[:], in_=ssq[:], func=AF.Sqrt, scale=1.0 / width, bias=epsr[:]),
                 reads=['ssq', 'epsr'], writes=['rstd'])
            P.op('dve', lambda e: e.reciprocal(out=rstd[:], in_=rstd[:]), reads=['rstd'], writes=['rstd'])
            P.op('dve', lambda e: e.scalar_tensor_tensor(out=out_t, in0=x_t, scalar=rstd[:, 0:1], in1=g_t,
                                                         op0=ALU.mult, op1=ALU.mult),
                 reads=[xname, 'rstd', gname], writes=[oname])

        def transpose_bf(src_t, sname, dst_t, dname, nchunks):
            k0 = 0
            while k0 < nchunks:
                n = min(8, nchunks - k0)
                p, pn = next_pt()
                for k in range(n):
                    P.op('pe', lambda e, k=k, p=p: e.transpose(out=p[:, k * 128:(k + 1) * 128],
                                                               in_=src_t[:, (k0 + k) * 128:(k0 + k + 1) * 128], identity=identb[:]),
                         reads=[sname, 'identb'], writes=[pn])
                P.op('act', lambda e, p=p, n=n: e.activation(out=dst_t[:, k0 * 128:(k0 + n) * 128], in_=p[:, 0:n * 128], func=AF.Copy),
                     reads=[pn], writes=[dname])
                k0 += n

        for l in range(depth):
            Xsrc = x_in if l == 0 else XR
            last = (l == depth - 1)
            with ExitStack() as ph:
                def sbp(name, shape, dt=F32):
                    return ph.enter_context(nc.sbuf_tensor(name + '_L%d' % l, list(shape), dt))
                Win = sbp("Win", [128, 8, INW], BF16)
                gmix = sbp("gmix", [128, D])
                xt = [sbp("p1x%d" % i, [128, D]) for i in range(2)]
                hb = sbp("p1h", [128, D], BF16)
                hT = sbp("p1hT", [128, D], BF16)
                tab = [sbp("p1tab%d" % i, [128, NTAB]) for i in range(2)]
                qk = sbp("p1qk", [128, 1280], BF16)
                qkT = sbp("p1qkT", [128, 1280], BF16)
                vat = [sbp("p1va%d" % i, [128, NHA * 65], BF16) for i in range(2)]
                vbt = [sbp("p1vb%d" % i, [128, NHB * 65], BF16) for i in range(2)]
                zct = [sbp("p1zc%d" % i, [128, CW]) for i in range(2)]
                tmp = [sbp("p1t%d" % i, [128, 96]) for i in range(4)]
                for i in range(2):
                    P.op('dve', lambda e, i=i: e.memset(vat[i][:], 1.0), writes=['p1va%d' % i])
                    P.op('dve', lambda e, i=i: e.memset(vbt[i][:], 1.0), writes=['p1vb%d' % i])
                load_weight(Win, 'Win', w_in[l], 8, INW)
                bcast_load(gmix[:], 'gmix', norm_mix[l:l + 1, :])
                ti = 0
                for s in range(NS):
                    for i in range(seq_lens[s] // 128):
                        g0 = offs[s] + i * 128
                        b = ti % 2
                        ti += 1
                        xn = 'p1x%d' % b
                        P.dma('sp', xt[b][:], Xsrc[g0:g0 + 128, :], reads=[('XR', g0)], writes=[xn])
                        P.dma('sp', tab[b][:], rope_tab[i * 128:(i + 1) * 128, :], writes=['p1tab%d' % b])
                        rmsnorm_tile(xt[b][:], xn, gmix[:], 'gmix', hb[:], 'p1h')
                        transpose_bf(hb, 'p1h', hT, 'p1hT', 8)
                        chunks = [(0, 384, 'qa'), (384, 384, 'ka'), (768, 384, 'va'), (1152, 512, 'qkb'), (1664, 256, 'vb'),
                                  (1920, 512, 'zc0'), (2432, 512, 'zc1'), (2944, 512, 'zc2')]
                        for (c0, w, kind) in chunks:
                            p, pn = next_pz()
                            for k in range(8):
                                P.op('pe', lambda e, p=p, k=k, c0=c0, w=w: e.matmul(p[:, 0:w], lhsT=hT[:, k * 128:(k + 1) * 128],
                                                                                     rhs=Win[:, k, c0:c0 + w], start=(k == 0), stop=(k == 7)),
                                     reads=['p1hT', 'Win'], writes=[pn])
                            if kind in ('qa', 'ka', 'qkb'):
                                if kind == 'qkb':
                                    nh, hd, half, qoff, toff = 16, 32, 4, 768, 192
                                else:
                                    nh, hd, half, toff = 6, 64, 8, 0
                                    qoff = 0 if kind == 'qa' else 384
                                nel = nh * half
                                src3 = p[:, 0:w].rearrange("p (h d) -> p h d", d=hd)
                                dst3 = qk[:, qoff:qoff + w].rearrange("p (h d) -> p h d", d=hd)
                                cscale = 1.0
                                if kind == 'qkb':
                                    cosv = tab[b][:, 192:256].rearrange("p (h d) -> p h d", d=half)
                                    sinv = tab[b][:, 256:320].rearrange("p (h d) -> p h d", d=half)
                                elif kind == 'ka':
                                    cosv = tab[b][:, 0:48].rearrange("p (h d) -> p h d", d=half)
                                    sinv = tab[b][:, 96:144].rearrange("p (h d) -> p h d", d=half)
                                else:
                                    cscale = 0.125
                                    cosv = tab[b][:, 48:96].rearrange("p (h d) -> p h d", d=half)
                                    sinv = tab[b][:, 144:192].rearrange("p (h d) -> p h d", d=half)
                                tn = 'p1tab%d' % b
                                P.op('act', lambda e, src3=src3, dst3=dst3, half=half, hd=hd: e.activation(
                                    out=dst3[:, :, 2 * half:hd], in_=src3[:, :, 2 * half:hd], func=AF.Copy, scale=cscale), reads=[pn], writes=['p1qk'])
                                x1 = src3[:, :, 0:half]
                                x2 = src3[:, :, half:2 * half]
                                tv = [tmp[j][:, 0:nel].rearrange("p (h d) -> p h d", d=half) for j in range(4)]
                                P.op('dve', lambda e, x1=x1, cosv=cosv, tv=tv: e.tensor_tensor(out=tv[0], in0=x1, in1=cosv, op=ALU.mult),
                                     reads=[pn, tn], writes=['p1t0'])
                                P.op('dve', lambda e, x2=x2, sinv=sinv, tv=tv: e.tensor_tensor(out=tv[1], in0=x2, in1=sinv, op=ALU.mult),
                                     reads=[pn, tn], writes=['p1t1'])
                                P.op('dve', lambda e, dst3=dst3, tv=tv, half=half: e.tensor_tensor(out=dst3[:, :, 0:half], in0=tv[0], in1=tv[1], op=ALU.subtract),
                                     reads=['p1t0', 'p1t1'], writes=['p1qk'])
                                P.op('dve', lambda e, x2=x2, cosv=cosv, tv=tv: e.tensor_tensor(out=tv[2], in0=x2, in1=cosv, op=ALU.mult),
                                     reads=[pn, tn], writes=['p1t2'])
                                P.op('dve', lambda e, x1=x1, sinv=sinv, tv=tv: e.tensor_tensor(out=tv[3], in0=x1, in1=sinv, op=ALU.mult),
                                     reads=[pn, tn], writes=['p1t3'])
                                P.op('dve', lambda e, dst3=dst3, tv=tv, half=half: e.tensor_tensor(out=dst3[:, :, half:2 * half], in0=tv[2], in1=tv[3], op=ALU.add),
                                     reads=['p1t2', 'p1t3'], writes=['p1qk'])
                            elif kind == 'va':
                                P.op('act', lambda e, p=p: e.activation(out=vat[b][:].rearrange("p (h d) -> p h d", d=65)[:, :, 0:64],
                                                                        in_=p[:, 0:384].rearrange("p (h d) -> p h d", d=64), func=AF.Copy),
                                     reads=[pn], writes=['p1va%d' % b])
                                P.dma('pool', VA[g0:g0 + 128, :], vat[b][:], reads=['p1va%d' % b], writes=[('VA', g0)])
                            elif kind == 'vb':
                                P.op('act', lambda e, p=p: e.activation(out=vbt[b][:].rearrange("p (h d) -> p h d", d=65)[:, :, 0:64],
                                                                        in_=p[:, 0:256].rearrange("p (h d) -> p h d", d=64), func=AF.Copy),
                                     reads=[pn], writes=['p1vb%d' % b])
                                P.dma('pool', VB[g0:g0 + 128, :], vbt[b][:], reads=['p1vb%d' % b], writes=[('VB', g0)])
                            else:
                                j = int(kind[2])
                                P.op('act', lambda e, p=p, j=j: e.activation(out=zct[b][:, j * 512:(j + 1) * 512], in_=p[:, 0:512], func=AF.Copy),
                                     reads=[pn], writes=['p1zc%d' % b])
                                if j == 2:
                                    zr = g0 + 2 * s + 1
                                    P.dma('pool', ZC[zr:zr + 128, :], zct[b][:], reads=['p1zc%d' % b], writes=[('ZC', g0)])
                        transpose_bf(qk, 'p1qk', qkT, 'p1qkT', 10)
                        P.dma('pool', QKT[:, g0:g0 + 128].rearrange("(c p) t -> p c t", p=128),
                              qkT[:].rearrange("p (c t) -> p c t", t=128), reads=['p1qkT'], writes=[('QKT', g0)])
                P.barrier()
                if stop_after == 1:
                    return nc


            with ExitStack() as ph:
                def sbp(name, shape, dt=F32):
                    return ph.enter_context(nc.sbuf_tensor(name + '_L%d' % l, list(shape), dt))
                mA = sbp("mA", [128, 20 * 512], BF16)
                P.dma('sp', mA[:], maskA[:, :], writes=['mA'])
                QT = [sbp("aQT%d" % i, [64, 6 * 512], BF16) for i in range(2)]
                KT = [sbp("aKT%d" % i, [64, 6 * 2560], BF16) for i in range(2)]
                VV = [sbp("aV%d" % i, [128, 20 * 390], BF16) for i in range(2)]
                pb = [sbp("aP%d" % i, [128, 512], BF16) for i in range(3)]
                pm = [sbp("aPM%d" % i, [128, 512], BF16) for i in range(3)]
                osb = [sbp("aO%d" % i, [65, 512]) for i in range(2)]
                otm = [sbp("aOT%d" % i, [128, 4 * 384], BF16) for i in range(2)]
                rec = sbp("aR", [128, 4])
                QKT64 = QKT.rearrange("(n d) t -> d n t", d=64)
                bi = 0
                pj = 0
                hj = 0
                for s in range(NS):
                    S = seq_lens[s]
                    for qb in range(S // 512):
                        g = offs[s] + qb * 512
                        b = bi % 2
                        bi += 1
                        kc_lo = max(0, 4 * qb - 8)
                        kc_hi = min(S // 128, 4 * qb + 12)
                        nkc = kc_hi - kc_lo
                        m_lo = kc_lo - (4 * qb - 8)
                        k0 = offs[s] + kc_lo * 128
                        k1 = offs[s] + kc_hi * 128
                        QTv = QT[b][:].rearrange("p (h t) -> p h t", t=512)
                        KTv = KT[b][:].rearrange("p (h t) -> p h t", t=2560)
                        VVv = VV[b][:].rearrange("p (c f) -> p c f", f=390)
                        OTv = otm[b][:].rearrange("p (j f) -> p j f", f=384)
                        P.dma('sp', QTv, QKT64[:, 0:6, g:g + 512], writes=['aQT%d' % b])
                        P.dma('sp', KTv[:, :, m_lo * 128:(m_lo + nkc) * 128], QKT64[:, 6:12, k0:k1], writes=['aKT%d' % b])
                        P.dma('sp', VVv[:, m_lo:m_lo + nkc, :], VA[k0:k1, :].rearrange("(c p) f -> p c f", p=128), writes=['aV%d' % b])
                        for h in range(NHA):
                            acc = pz[4 + hj % 2]
                            accn = 'pz%d' % (4 + hj % 2)
                            oj = hj % 2
                            hj += 1
                            for m in range(m_lo, m_lo + nkc):
                                p, pn = next_pz()
                                j = pj % 3
                                pj += 1
                                P.op('pe', lambda e: e.matmul(p[:, :], lhsT=KTv[:, h, m * 128:(m + 1) * 128], rhs=QTv[:, h, :], start=True, stop=False),
                                     reads=['aKT%d' % b, 'aQT%d' % b], writes=[pn])
                                P.op('pe', lambda e: e.matmul(p[:, :], lhsT=identb[:], rhs=mA[:, m * 512:(m + 1) * 512], start=False, stop=True),
                                     reads=['identb', 'mA'], writes=[pn])
                                P.op('act', lambda e: e.activation(out=pb[j][:], in_=p[:, :], func=AF.Exp),
                                     reads=[pn], writes=['aP%d' % j])
                                P.op('pe', lambda e: e.matmul(acc[0:65, :], lhsT=VVv[:, m, h * 65:(h + 1) * 65], rhs=pb[j][:],
                                                              start=(m == m_lo), stop=(m == m_lo + nkc - 1)),
                                     reads=['aV%d' % b, 'aP%d' % j], writes=[accn])
                            P.op('dve', lambda e: e.tensor_copy(out=osb[oj][:], in_=acc[0:65, :]), reads=[accn], writes=['aO%d' % oj])
                            p, pn = next_pz()
                            for j4 in range(4):
                                P.op('pe', lambda e: e.transpose(out=p[:, j4 * 65:(j4 + 1) * 65], in_=osb[oj][:, j4 * 128:(j4 + 1) * 128],
                                                                 identity=identf[0:65, 0:65]), reads=['aO%d' % oj, 'identf'], writes=[pn])
                            pv = p[:, 0:260].rearrange("p (j f) -> p j f", f=65)
                            P.op('dve', lambda e: e.reciprocal(out=rec[:, 0:4], in_=pv[:, :, 64]), reads=[pn], writes=['aR'])
                            for j4 in range(4):
                                P.op('dve', lambda e: e.tensor_scalar(out=OTv[:, j4, h * 64:(h + 1) * 64], in0=p[:, j4 * 65:j4 * 65 + 64],
                                                                      scalar1=rec[:, j4:j4 + 1], scalar2=None, op0=ALU.mult),
                                     reads=[pn, 'aR'], writes=['aOT%d' % b])
                        P.dma('pool', OS[g:g + 512, 0:384].rearrange("(j p) f -> p j f", p=128), OTv, reads=['aOT%d' % b], writes=[('OSA', g)])
                P.barrier()
                if stop_after == 2:
                    return nc

            lam_init = 0.8 - 0.6 * math.exp(-0.3 * l)
            with ExitStack() as ph:
                def sbp(name, shape, dt=F32):
                    return ph.enter_context(nc.sbuf_tensor(name + '_L%d' % l, list(shape), dt))
                lamq = sbp("lamq", [128, 128])
                lt = sbp("lamt", [128, 128])
                ls = sbp("lams", [128, 4])
                neglam = sbp("neglam", [128, 1])
                subw = sbp("subw", [128, 64])
                epss = sbp("epss", [128, 1])
                KT2 = sbp("bKT", [32, 2 * SMAX], BF16)
                VB2 = sbp("bV", [128, (SMAX // 128) * 65], BF16)
                QT2 = [sbp("bQT%d" % i, [32, 2 * 512], BF16) for i in range(2)]
                pb = [sbp("bP%d" % i, [128, 1024], BF16) for i in range(2)]
                osb = [sbp("bO%d" % i, [65, 512]) for i in range(2)]
                rec = sbp("bR", [128, 8])
                ob0 = sbp("bob0", [128, 256])
                ob1 = sbp("bob1", [128, 256])
                sq = sbp("bsq", [128, 256])
                ss4 = sbp("bss4", [128, 4])
                otb = [sbp("bOT%d" % i, [128, 256], BF16) for i in range(2)]
                bcast_load(lamq[:], 'lamq', lam_qk[l:l + 1, :])
                bcast_load(subw[:], 'subw', subln_w[l:l + 1, :])
                P.op('dve', lambda e: e.memset(epss[:], SUBLN_EPS), writes=['epss'])
                P.op('dve', lambda e: e.scalar_tensor_tensor(out=lt[:, 0:32], in0=lamq[:, 0:32], scalar=1.0, in1=lamq[:, 32:64],
                                                             op0=ALU.mult, op1=ALU.mult, accum_out=ls[:, 0:1]), reads=['lamq'], writes=['lamt', 'lams'])
                P.op('dve', lambda e: e.scalar_tensor_tensor(out=lt[:, 32:64], in0=lamq[:, 64:96], scalar=1.0, in1=lamq[:, 96:128],
                                                             op0=ALU.mult, op1=ALU.mult, accum_out=ls[:, 1:2]), reads=['lamq', 'lams'], writes=['lamt', 'lams'])
                P.op('act', lambda e: e.activation(out=ls[:, 2:4], in_=ls[:, 0:2], func=AF.Exp), reads=['lams'], writes=['lams'])
                P.op('dve', lambda e: e.tensor_tensor(out=neglam[:], in0=ls[:, 3:4], in1=ls[:, 2:3], op=ALU.subtract), reads=['lams'], writes=['neglam'])
                P.op('dve', lambda e: e.tensor_scalar(out=neglam[:], in0=neglam[:], scalar1=-lam_init, scalar2=None, op0=ALU.add),
                     reads=['neglam'], writes=['neglam'])
                P.op('dve', lambda e: e.tensor_scalar(out=subw[:], in0=subw[:], scalar1=1.0 - lam_init, scalar2=None, op0=ALU.mult),
                     reads=['subw'], writes=['subw'])
                QKB32 = QKT[768:1280, :].rearrange("(n d) t -> d n t", d=32)
                bi = 0
                pj = 0
                for s in range(NS):
                    S = seq_lens[s]
                    nkc = S // 128
                    KTv = KT2[:, 0:2 * S].rearrange("p (m t) -> p m t", t=S)
                    Vv = VB2[:, 0:nkc * 65].rearrange("p (c f) -> p c f", f=65)
                    for h in range(NHB):
                        P.dma('sp', KTv, QKB32[:, 8 + 2 * h:10 + 2 * h, offs[s]:offs[s] + S], writes=['bKT'])
                        P.dma('sp', Vv, VB[offs[s]:offs[s] + S, h * 65:(h + 1) * 65].rearrange("(c p) f -> p c f", p=128), writes=['bV'])
                        for qb in range(S // 512):
                            g = offs[s] + qb * 512
                            b = bi % 2
                            bi += 1
                            QTv = QT2[b][:].rearrange("p (m t) -> p m t", t=512)
                            P.dma('sp', QTv, QKB32[:, 2 * h:2 * h + 2, g:g + 512], writes=['bQT%d' % b])
                            for mm in range(2):
                                acc = pz[4]
                                accn = 'pz4'
                                for kg in range(nkc // 2):
                                    gi = pj % 2
                                    pj += 1
                                    pg = pzbig[gi]
                                    pgn = 'pzb%d' % gi
                                    for u in range(2):
                                        kc = kg * 2 + u
                                        P.op('pe', lambda e: e.matmul(pg[:, u * 512:(u + 1) * 512], lhsT=KTv[:, mm, kc * 128:(kc + 1) * 128], rhs=QTv[:, mm, :],
                                                                      start=True, stop=True), reads=['bKT', 'bQT%d' % b], writes=[pgn])
                                    P.op('act', lambda e: e.activation(out=pb[gi][:], in_=pg[:, :], func=AF.Exp, scale=32 ** -0.5),
                                         reads=[pgn], writes=['bP%d' % gi])
                                    for u in range(2):
                                        kc = kg * 2 + u
                                        P.op('pe', lambda e: e.matmul(acc[0:65, :], lhsT=Vv[:, kc, :], rhs=pb[gi][:, u * 512:(u + 1) * 512],
                                                                      start=(kc == 0), stop=(kc == nkc - 1)), reads=['bV', 'bP%d' % gi], writes=[accn])
                                P.op('dve', lambda e: e.tensor_copy(out=osb[mm][:], in_=acc[0:65, :]), reads=[accn], writes=['bO%d' % mm])
                                p, pn = pz[5], 'pz5'
                                for j4 in range(4):
                                    P.op('pe', lambda e: e.transpose(out=p[:, j4 * 65:(j4 + 1) * 65], in_=osb[mm][:, j4 * 128:(j4 + 1) * 128],
                                                                     identity=identf[0:65, 0:65]), reads=['bO%d' % mm, 'identf'], writes=[pn])
                                pv = p[:, 0:260].rearrange("p (j f) -> p j f", f=65)
                                P.op('dve', lambda e: e.reciprocal(out=rec[:, mm * 4:mm * 4 + 4], in_=pv[:, :, 64]), reads=[pn], writes=['bR'])
                                for j4 in range(4):
                                    if mm == 0:
                                        P.op('dve', lambda e: e.tensor_scalar(out=ob0[:, j4 * 64:(j4 + 1) * 64], in0=p[:, j4 * 65:j4 * 65 + 64],
                                                                              scalar1=rec[:, j4:j4 + 1], scalar2=None, op0=ALU.mult),
                                             reads=[pn, 'bR'], writes=['bob0'])
                                    else:
                                        P.op('dve', lambda e: e.tensor_scalar(out=ob1[:, j4 * 64:(j4 + 1) * 64], in0=p[:, j4 * 65:j4 * 65 + 64],
                                                                              scalar1=rec[:, 4 + j4:5 + j4], scalar2=neglam[:, 0:1], op0=ALU.mult, op1=ALU.mult),
                                             reads=[pn, 'bR', 'neglam'], writes=['bob1'])
                            P.op('dve', lambda e: e.tensor_tensor(out=ob0[:], in0=ob0[:], in1=ob1[:], op=ALU.add), reads=['bob0', 'bob1'], writes=['bob0'])
                            P.op('dve', lambda e: e.tensor_tensor(out=sq[:], in0=ob0[:], in1=ob0[:], op=ALU.mult), reads=['bob0'], writes=['bsq'])
                            P.op('dve', lambda e: e.tensor_reduce(out=ss4[:], in_=sq[:].rearrange("p (j f) -> p j f", f=64), axis=AX.X, op=ALU.add),
                                 reads=['bsq'], writes=['bss4'])
                            P.op('act', lambda e: e.activation(out=ss4[:], in_=ss4[:], func=AF.Sqrt, scale=1.0 / 64, bias=epss[:]),
                                 reads=['bss4', 'epss'], writes=['bss4'])
                            P.op('dve', lambda e: e.reciprocal(out=ss4[:], in_=ss4[:]), reads=['bss4'], writes=['bss4'])
                            for j4 in range(4):
                                P.op('dve', lambda e: e.scalar_tensor_tensor(out=otb[b][:, j4 * 64:(j4 + 1) * 64], in0=ob0[:, j4 * 64:(j4 + 1) * 64],
                                                                             scalar=ss4[:, j4:j4 + 1], in1=subw[:], op0=ALU.mult, op1=ALU.mult),
                                     reads=['bob0', 'bss4', 'subw'], writes=['bOT%d' % b])
                            P.dma('pool', OS[g:g + 512, 384 + h * 64:384 + (h + 1) * 64].rearrange("(j p) f -> p j f", p=128),
                                  otb[b][:].rearrange("p (j f) -> p j f", f=64), reads=['bOT%d' % b], writes=[('OSB', g, h)])
                P.barrier()
                if stop_after == 3:
                    return nc


            with ExitStack() as ph:
                def sbp(name, shape, dt=F32):
                    return ph.enter_context(nc.sbuf_tensor(name + '_L%d' % l, list(shape), dt))
                mu_t = sbp("mu_t", [128, CW])
                w0_t = sbp("w0_t", [128, 2 * WC])
                a0_t = sbp("a0_t", [128, 2 * WC])
                kkc = sbp("kkc", [128, WC])
                kac = sbp("kac", [128, WC])
                rkc = sbp("rkc", [128, WC])
                wup = sbp("wup", [128, WC])
                aup = sbp("aup", [128, WC])
                gup = sbp("gup", [128, WC])
                bcast_load(mu_t[:], 'mu_t', mu_c[l:l + 1, :])
                bcast_load(w0_t[:], 'w0_t', w0[l:l + 1, :])
                bcast_load(a0_t[:], 'a0_t', a0[l:l + 1, :])
                bcast_load(kkc[:], 'kkc', k_k[l:l + 1, :])
                bcast_load(kac[:], 'kac', k_a[l:l + 1, :])
                bcast_load(rkc[:], 'rkc', r_k[l:l + 1, :])
                P.dma('sp', wup[:], w_up[l].rearrange("d r c -> (d r) c"), writes=['wup'])
                P.dma('sp', aup[:], a_up[l].rearrange("d r c -> (d r) c"), writes=['aup'])
                P.dma('sp', gup[:], g_up[l], writes=['gup'])
                if l > 0:
                    v0c = sbp("v0c", [128, WC])
                    vdn = sbp("vdn", [128, 3 * 32])
                    vupt = sbp("vupt", [32, WC])
                    vcT = sbp("vcT", [128, WC])
                    vdT = sbp("vdT", [32, 128])
                    vft = sbp("vft", [128, WC])
                    bcast_load(v0c[:], 'v0c', v0[l - 1:l, :])
                    P.dma('sp', vdn[:].rearrange("p (c r) -> p c r", r=32), v_down[l - 1].rearrange("(c p) r -> p c r", p=128), writes=['vdn'])
                    P.dma('sp', vupt[:], v_up[l - 1], writes=['vupt'])
                zp = sbp("zp", [128, CW])
                zm = sbp("zm", [128, CW])
                zn = sbp("zn", [128, CW])
                lt_in = sbp("lt_in", [128, 384])
                ltT = sbp("ltT", [128, 384])
                rwt = sbp("rwt", [128, 9 * WC])
                gbt = sbp("gbt", [128, 2 * WC])
                tA = sbp("tA", [128, WC])
                tB = sbp("tB", [128, WC])
                tC = sbp("tC", [128, WC])
                kka = sbp("kka", [128, WC])
                rr = sbp("rr", [128, WC])
                s6 = sbp("s6", [128, 24])
                for s in range(NS):
                    for i in range(seq_lens[s] // 128):
                        g0 = offs[s] + i * 128
                        zr = g0 + 2 * s
                        P.dma('sp', zp[:], ZC[zr:zr + 128, :], writes=['zp'])
                        P.dma('sp', zm[:], ZC[zr + 1:zr + 129, :], writes=['zm'])
                        P.dma('sp', zn[:], ZC[zr + 2:zr + 130, :], writes=['zn'])
                        P.op('pool', lambda e: e.tensor_tensor(out=zp[:], in0=zp[:], in1=zn[:], op=ALU.add), reads=['zp', 'zn'], writes=['zp'])
                        P.op('dve', lambda e: e.scalar_tensor_tensor(out=zp[:], in0=zp[:], scalar=0.5, in1=zm[:], op0=ALU.mult, op1=ALU.subtract),
                             reads=['zp', 'zm'], writes=['zp'])
                        P.op('dve', lambda e: e.tensor_tensor(out=zp[:], in0=zp[:], in1=mu_t[:], op=ALU.mult), reads=['zp', 'mu_t'], writes=['zp'])
                        P.op('dve', lambda e: e.tensor_tensor(out=zp[:], in0=zp[:], in1=zm[:], op=ALU.add), reads=['zp', 'zm'], writes=['zp'])
                        r_ = zp[:, 0:384]
                        kc_ = zp[:, 384:768]
                        vc_ = zp[:, 768:1152]
                        P.op('act', lambda e: e.activation(out=lt_in[:, 0:128], in_=zp[:, 1152:1280], func=AF.Tanh), reads=['zp'], writes=['lt_in'])
                        P.op('act', lambda e: e.activation(out=lt_in[:, 256:384], in_=zp[:, 1408:1536], func=AF.Sigmoid), reads=['zp'], writes=['lt_in'])
                        P.op('act', lambda e: e.activation(out=lt_in[:, 128:256], in_=zp[:, 1280:1408], func=AF.Copy), reads=['zp'], writes=['lt_in'])
                        p, pn = next_pz()
                        for j in range(3):
                            P.op('pe', lambda e: e.transpose(out=p[:, j * 128:(j + 1) * 128], in_=lt_in[:, j * 128:(j + 1) * 128], identity=identf[:]),
                                 reads=['lt_in', 'identf'], writes=[pn])
                        P.op('act', lambda e: e.activation(out=ltT[:], in_=p[:, 0:384], func=AF.Copy), reads=[pn], writes=['ltT'])
                        P.op('dve', lambda e: e.tensor_tensor(out=tA[:], in0=kc_, in1=kkc[:], op=ALU.mult), reads=['zp', 'kkc'], writes=['tA'])
                        P.op('dve', lambda e: e.tensor_tensor(out=tB[:], in0=tA[:], in1=tA[:], op=ALU.mult), reads=['tA'], writes=['tB'])
                        P.op('dve', lambda e: e.tensor_reduce(out=s6[:, 0:6], in_=tB[:].rearrange("p (h d) -> p h d", d=64), axis=AX.X, op=ALU.add),
                             reads=['tB'], writes=['s6'])
                        P.op('act', lambda e: e.activation(out=s6[:, 0:6], in_=s6[:, 0:6], func=AF.Sqrt), reads=['s6'], writes=['s6'])
                        P.op('dve', lambda e: e.tensor_scalar(out=s6[:, 0:6], in0=s6[:, 0:6], scalar1=1e-12, scalar2=None, op0=ALU.max), reads=['s6'], writes=['s6'])
                        P.op('dve', lambda e: e.reciprocal(out=s6[:, 0:6], in_=s6[:, 0:6]), reads=['s6'], writes=['s6'])
                        for h in range(6):
                            P.op('dve', lambda e: e.tensor_scalar(out=rwt[:, 2 * WC + h * 64:2 * WC + (h + 1) * 64], in0=tA[:, h * 64:(h + 1) * 64],
                                                                  scalar1=s6[:, h:h + 1], scalar2=None, op0=ALU.mult), reads=['tA', 's6'], writes=['rwt'])
                        P.op('dve', lambda e: e.tensor_tensor(out=kka[:], in0=kc_, in1=kac[:], op=ALU.mult), reads=['zp', 'kac'], writes=['kka'])
                        P.op('dve', lambda e: e.tensor_tensor(out=rr[:], in0=r_, in1=rkc[:], op=ALU.mult), reads=['zp', 'rkc'], writes=['rr'])
                        P.op('act', lambda e: e.activation(out=rwt[:, 0:WC], in_=r_, func=AF.Copy), reads=['zp'], writes=['rwt'])
                        if l == 0:
                            P.op('act', lambda e: e.activation(out=rwt[:, WC:2 * WC], in_=vc_, func=AF.Copy), reads=['zp'], writes=['rwt'])
                            P.dma('pool', VF[g0:g0 + 128, :], rwt[:, WC:2 * WC], reads=['rwt'], writes=[('VF', g0)])
                        else:
                            P.dma('sp', vft[:], VF[g0:g0 + 128, :], writes=['vft'])
                            p, pn = next_pz()
                            for j in range(3):
                                P.op('pe', lambda e: e.transpose(out=p[:, j * 128:(j + 1) * 128], in_=zp[:, 768 + j * 128:768 + (j + 1) * 128], identity=identf[:]),
                                     reads=['zp', 'identf'], writes=[pn])
                            P.op('act', lambda e: e.activation(out=vcT[:], in_=p[:, 0:384], func=AF.Copy), reads=[pn], writes=['vcT'])
                            p, pn = next_pz()
                            for j in range(3):
                                P.op('pe', lambda e: e.matmul(p[0:32, 0:128], lhsT=vdn[:, j * 32:(j + 1) * 32], rhs=vcT[:, j * 128:(j + 1) * 128],
                                                              start=(j == 0), stop=(j == 2)), reads=['vdn', 'vcT'], writes=[pn])
                            P.op('act', lambda e: e.activation(out=vdT[:], in_=p[0:32, 0:128], func=AF.Copy), reads=[pn], writes=['vdT'])
                            p, pn = next_pz()
                            P.op('pe', lambda e: e.matmul(p[:, 0:384], lhsT=vdT[:], rhs=vupt[:], start=True, stop=True), reads=['vdT', 'vupt'], writes=[pn])
                            P.op('dve', lambda e: e.tensor_tensor(out=tB[:], in0=p[:, 0:384], in1=v0c[:], op=ALU.add), reads=[pn, 'v0c'], writes=['tB'])
                            P.op('act', lambda e: e.activation(out=tB[:], in_=tB[:], func=AF.Sigmoid), reads=['tB'], writes=['tB'])
                            P.op('dve', lambda e: e.tensor_tensor(out=tC[:], in0=vft[:], in1=vc_, op=ALU.subtract), reads=['vft', 'zp'], writes=['tC'])
                            P.op('dve', lambda e: e.tensor_tensor(out=tC[:], in0=tC[:], in1=tB[:], op=ALU.mult), reads=['tC', 'tB'], writes=['tC'])
                            P.op('dve', lambda e: e.tensor_tensor(out=rwt[:, WC:2 * WC], in0=tC[:], in1=vc_, op=ALU.add), reads=['tC', 'zp'], writes=['rwt'])
                        for d in range(2):
                            pw, pwn = next_pz()
                            P.op('pe', lambda e: e.matmul(pw[:, 0:384], lhsT=ltT[d * 64:(d + 1) * 64, 0:128], rhs=wup[d * 64:(d + 1) * 64, :], start=True, stop=True),
                                 reads=['ltT', 'wup'], writes=[pwn])
                            pa, pan = next_pz()
                            P.op('pe', lambda e: e.matmul(pa[:, 0:384], lhsT=ltT[d * 64:(d + 1) * 64, 128:256], rhs=aup[d * 64:(d + 1) * 64, :], start=True, stop=True),
                                 reads=['ltT', 'aup'], writes=[pan])
                            fo = (3 + 3 * d) * WC
                            P.op('dve', lambda e: e.tensor_tensor(out=tB[:], in0=pw[:, 0:384], in1=w0_t[:, d * WC:(d + 1) * WC], op=ALU.add),
                                 reads=[pwn, 'w0_t'], writes=['tB'])
                            P.op('act', lambda e: e.activation(out=tB[:], in_=tB[:], func=AF.Sigmoid), reads=['tB'], writes=['tB'])
                            P.op('dve', lambda e: e.tensor_scalar(out=rwt[:, fo:fo + WC], in0=tB[:], scalar1=-math.exp(-0.5), scalar2=None, op0=ALU.mult),
                                 reads=['tB'], writes=['rwt'])
                            P.op('dve', lambda e: e.tensor_tensor(out=tC[:], in0=pa[:, 0:384], in1=a0_t[:, d * WC:(d + 1) * WC], op=ALU.add),
                                 reads=[pan, 'a0_t'], writes=['tC'])
                            P.op('act', lambda e: e.activation(out=tC[:], in_=tC[:], func=AF.Sigmoid), reads=['tC'], writes=['tC'])
                            P.op('dve', lambda e: e.tensor_tensor(out=rwt[:, fo + 2 * WC:fo + 3 * WC], in0=rwt[:, 2 * WC:3 * WC], in1=tC[:], op=ALU.mult),
                                 reads=['rwt', 'tC'], writes=['rwt'])
                            P.op('dve', lambda e: e.scalar_tensor_tensor(out=tB[:], in0=tC[:], scalar=-1.0, in1=kka[:], op0=ALU.add, op1=ALU.mult),
                                 reads=['tC', 'kka'], writes=['tB'])
                            P.op('dve', lambda e: e.tensor_tensor(out=rwt[:, fo + WC:fo + 2 * WC], in0=tB[:], in1=kc_, op=ALU.add),
                                 reads=['tB', 'zp'], writes=['rwt'])
                            P.op('dve', lambda e: e.tensor_tensor(out=tB[:], in0=rr[:], in1=rwt[:, fo + WC:fo + 2 * WC], op=ALU.mult),
                                 reads=['rr', 'rwt'], writes=['tB'])
                            P.op('dve', lambda e: e.tensor_reduce(out=s6[:, 6 + 6 * d:12 + 6 * d], in_=tB[:].rearrange("p (h d) -> p h d", d=64), axis=AX.X, op=ALU.add),
                                 reads=['tB'], writes=['s6'])
                        pg, pgn = next_pz()
                        P.op('pe', lambda e: e.matmul(pg[:, 0:384], lhsT=ltT[:, 256:384], rhs=gup[:], start=True, stop=True), reads=['ltT', 'gup'], writes=[pgn])
                        P.op('act', lambda e: e.activation(out=gbt[:, 0:WC], in_=pg[:, 0:384], func=AF.Copy), reads=[pgn], writes=['gbt'])
                        P.op('dve', lambda e: e.tensor_tensor(out=s6[:, 18:24], in0=s6[:, 6:12], in1=s6[:, 12:18], op=ALU.add), reads=['s6'], writes=['s6'])
                        for h in range(6):
                            P.op('dve', lambda e: e.tensor_scalar(out=gbt[:, WC + h * 64:WC + (h + 1) * 64], in0=rwt[:, WC + h * 64:WC + (h + 1) * 64],
                                                                  scalar1=s6[:, 18 + h:19 + h], scalar2=None, op0=ALU.mult), reads=['rwt', 's6'], writes=['gbt'])
                        P.dma('pool', RW[g0:g0 + 128, :], rwt[:], reads=['rwt'], writes=[('RW', g0)])
                        P.dma('pool', GB[g0:g0 + 128, :], gbt[:], reads=['gbt'], writes=[('GB', g0)])
                P.barrier()
                if stop_after == 4:
                    return nc

            with ExitStack() as ph:
                def sbp(name, shape, dt=F32):
                    return ph.enter_context(nc.sbuf_tensor(name + '_L%d' % l, list(shape), dt))
                tri = sbp("tri", [128, 512])
                mkc = sbp("mkc", [128, 1792])
                ones1 = sbp("ones1", [128, 2])
                lnw = sbp("lnw", [128, WC])
                lnb = sbp("lnb", [128, WC])
                epsl = sbp("epsl", [128, 1])
                P.dma('sp', tri[:], tri_c[:, :], writes=['tri'])
                P.dma('sp', mkc[:], mask_c[:, :], writes=['mkc'])
                P.op('dve', lambda e: e.memset(ones1[:], 1.0), writes=['ones1'])
                P.op('dve', lambda e: e.memset(epsl[:], LNX_EPS), writes=['epsl'])
                bcast_load(lnw[:], 'lnw', lnx_w[l:l + 1, :])
                bcast_load(lnb[:], 'lnb', lnx_b[l:l + 1, :])
                rw = [sbp("rw%d" % i, [128, 6 * WC]) for i in range(2)]
                ee = sbp("ee", [128, 4 * WC])
                A4 = sbp("A4", [128, 4 * WC], BF16)
                BH = sbp("BH", [128, WC], BF16)
                KH = sbp("KH", [128, WC], BF16)
                V16 = sbp("V16", [128, WC], BF16)
                wc = sbp("wc", [64, 6])
                FM = sbp("FM", [64, 24 * 128], BF16)
                AT = sbp("AT", [128, 6 * 512], BF16)
                NSq = [sbp("NSq%d" % i, [128, 6 * 128]) for i in range(7)]
                NTS = [sbp("NTS%d" % i, [128, 6 * 128]) for i in range(3)]
                U32 = sbp("U32", [128, WC])
                U16 = [sbp("U16_%d" % i, [128, WC], BF16) for i in range(2)]
                SF = sbp("SF", [64, WC])
                S16 = sbp("S16", [64, WC], BF16)
                ych = [sbp("ych%d" % i, [128, WC]) for i in range(2)]
                yft = sbp("yft", [128, WC])
                gbc = sbp("gbc", [128, 2 * WC])
                yc = sbp("yc", [128, WC])
                ysq = sbp("ysq", [128, WC])
                st6 = sbp("st6", [128, 12])
                ocb = [sbp("ocb%d" % i, [128, WC], BF16) for i in range(2)]

                ci = 0
                for d in range(2):
                    mo = d * 896
                    for s in range(NS):
                        S = seq_lens[s]
                        nch = S // 128
                        P.op('dve', lambda e: e.memset(SF[:], 0.0), writes=['SF'])
                        P.op('dve', lambda e: e.memset(S16[:], 0.0), writes=['S16'])
                        order = range(nch) if d == 0 else range(nch - 1, -1, -1)
                        for c in order:
                            t0 = offs[s] + c * 128
                            b = ci % 2
                            ci += 1
                            rwn = 'rw%d' % b
                            fo = (3 + 3 * d) * WC
                            P.dma('sp', rw[b][:, 0:3 * WC], RW[t0:t0 + 128, 0:3 * WC], writes=[rwn])
                            P.dma('sp', rw[b][:, 3 * WC:6 * WC], RW[t0:t0 + 128, fo:fo + 3 * WC], writes=[rwn])
                            r_ = rw[b][:, 0:WC]
                            v_ = rw[b][:, WC:2 * WC]
                            kk_ = rw[b][:, 2 * WC:3 * WC]
                            lw_ = rw[b][:, 3 * WC:4 * WC]
                            kd_ = rw[b][:, 4 * WC:5 * WC]
                            b_ = rw[b][:, 5 * WC:6 * WC]
                            pL1, pL1n = next_pz()
                            P.op('pe', lambda e: e.matmul(pL1[:, 0:384], lhsT=tri[:, (2 * d) * 128:(2 * d + 1) * 128], rhs=lw_, start=True, stop=True),
                                 reads=['tri', rwn], writes=[pL1n])
                            pL2, pL2n = next_pz()
                            P.op('pe', lambda e: e.matmul(pL2[:, 0:384], lhsT=tri[:, (2 * d + 1) * 128:(2 * d + 2) * 128], rhs=lw_, start=True, stop=True),
                                 reads=['tri', rwn], writes=[pL2n])
                            pW, pWn = next_pz()
                            for h in range(6):
                                P.op('pe', lambda e: e.matmul(pW[0:64, 2 * h:2 * h + 2], lhsT=lw_[:, h * 64:(h + 1) * 64], rhs=ones1[:, 0:2], start=True, stop=True),
                                     reads=[rwn, 'ones1'], writes=[pWn])
                            P.op('act', lambda e: e.activation(out=wc[:], in_=pW[0:64, 0:12].rearrange("p (h two) -> p h two", two=2)[:, :, 0], func=AF.Exp),
                                 reads=[pWn], writes=['wc'])
                            P.op('act', lambda e: e.activation(out=ee[:, 0:WC], in_=pL1[:, 0:384], func=AF.Exp), reads=[pL1n], writes=['ee0'])
                            P.op('act', lambda e: e.activation(out=ee[:, WC:2 * WC], in_=pL1[:, 0:384], func=AF.Exp, scale=-1.0), reads=[pL1n], writes=['ee1'])
                            P.op('dve', lambda e: e.tensor_tensor(out=ee[:, 2 * WC:3 * WC], in0=pL1[:, 0:384], in1=lw_, op=ALU.subtract),
                                 reads=[pL1n, rwn], writes=['ee2'])
                            P.op('act', lambda e: e.activation(out=ee[:, 2 * WC:3 * WC], in_=ee[:, 2 * WC:3 * WC], func=AF.Exp), reads=['ee2'], writes=['ee2'])
                            P.op('act', lambda e: e.activation(out=ee[:, 3 * WC:4 * WC], in_=pL2[:, 0:384], func=AF.Exp), reads=[pL2n], writes=['ee3'])
                            P.op('dve', lambda e: e.scalar_tensor_tensor(out=A4[:, 0:WC], in0=kk_, scalar=-1.0, in1=ee[:, 2 * WC:3 * WC], op0=ALU.mult, op1=ALU.mult),
                                 reads=[rwn, 'ee2'], writes=['A4'])
                            P.op('dve', lambda e: e.tensor_tensor(out=A4[:, WC:2 * WC], in0=r_, in1=ee[:, 0:WC], op=ALU.mult), reads=[rwn, 'ee0'], writes=['A4'])
                            P.op('dve', lambda e: e.tensor_tensor(out=A4[:, 2 * WC:3 * WC], in0=b_, in1=ee[:, WC:2 * WC], op=ALU.mult), reads=[rwn, 'ee1'], writes=['A4'])
                            P.op('dve', lambda e: e.tensor_tensor(out=A4[:, 3 * WC:4 * WC], in0=kd_, in1=ee[:, WC:2 * WC], op=ALU.mult), reads=[rwn, 'ee1'], writes=['A4'])
                            P.op('pool', lambda e: e.tensor_tensor(out=BH[:], in0=b_, in1=ee[:, 3 * WC:4 * WC], op=ALU.mult), reads=[rwn, 'ee3'], writes=['BH'])
                            P.op('pool', lambda e: e.tensor_tensor(out=KH[:], in0=kd_, in1=ee[:, 3 * WC:4 * WC], op=ALU.mult), reads=[rwn, 'ee3'], writes=['KH'])
                            P.op('pool', lambda e: e.tensor_copy(out=V16[:], in_=v_), reads=[rwn], writes=['V16'])
                            for g3 in range(3):
                                p, pn = next_pt()
                                for u in range(8):
                                    idx = g3 * 8 + u
                                    h, q = idx // 4, idx % 4
                                    P.op('pe', lambda e: e.transpose(out=p[0:64, u * 128:(u + 1) * 128], in_=A4[:, q * WC + h * 64:q * WC + (h + 1) * 64],
                                                                     identity=identb[:]), reads=['A4', 'identb'], writes=[pn])
                                P.op('act', lambda e: e.activation(out=FM[:, g3 * 1024:(g3 + 1) * 1024], in_=p[0:64, 0:1024], func=AF.Copy),
                                     reads=[pn], writes=[('FM', g3)])

                            def fm(h, q):
                                o = (h * 4 + q) * 128
                                return FM[:, o:o + 128]
                            for h in range(6):
                                p, pn = next_pz()
                                P.op('pe', lambda e: e.matmul(p[:, 0:256], lhsT=fm(h, 2), rhs=FM[:, h * 512:h * 512 + 256], start=True, stop=True),
                                     reads=[('FM', h // 2)], writes=[pn])
                                P.op('pe', lambda e: e.matmul(p[:, 256:512], lhsT=fm(h, 3), rhs=FM[:, h * 512:h * 512 + 256], start=True, stop=True),
                                     reads=[('FM', h // 2)], writes=[pn])
                                P.op('dve', lambda e: e.tensor_tensor(out=AT[:, h * 512:(h + 1) * 512], in0=p[:, 0:512], in1=mkc[:, mo:mo + 512], op=ALU.mult),
                                     reads=[pn, 'mkc'], writes=[('AT', h)])
                                P.op('dve', lambda e: e.tensor_tensor(out=NSq[0][:, h * 128:(h + 1) * 128], in0=p[:, 0:128], in1=mkc[:, mo:mo + 128], op=ALU.mult),
                                     reads=[pn, 'mkc'], writes=[('N', 0, h)])
                            for g2 in range(2):
                                p, pn = next_pz()
                                for hh in range(3):
                                    h = g2 * 3 + hh
                                    P.op('pe', lambda e: e.matmul(p[:, hh * 128:(hh + 1) * 128], lhsT=fm(h, 0), rhs=fm(h, 2), start=True, stop=True),
                                         reads=[('FM', h // 2)], writes=[pn])
                                P.op('dve', lambda e: e.tensor_tensor(out=NTS[0][:, g2 * 384:(g2 + 1) * 384], in0=p[:, 0:384], in1=mkc[:, mo + 512:mo + 896], op=ALU.mult),
                                     reads=[pn, 'mkc'], writes=[('NT', 0, g2)])

                            def Nj(j, h):
                                if j == 0:
                                    return NSq[0][:, h * 128:(h + 1) * 128], ('N', 0, h)
                                return NSq[j][:, h * 128:(h + 1) * 128], ('N', j, h // 3)

                            def nti(j):
                                return 0 if j == 0 else 1 + (j % 2)

                            def NTj(j, h):
                                return NTS[nti(j)][:, h * 128:(h + 1) * 128], ('NT', nti(j), h // 3)
                            for j in range(1, 7):
                                for g2 in range(2):
                                    p, pn = next_pz()
                                    for hh in range(3):
                                        h = g2 * 3 + hh
                                        n_, nn = Nj(j - 1, h)
                                        nt_, ntn = NTj(j - 1, h)
                                        P.op('pe', lambda e: e.matmul(p[:, hh * 128:(hh + 1) * 128], lhsT=nt_, rhs=n_, start=True, stop=True),
                                             reads=[nn, ntn], writes=[pn])
                                    P.op('act', lambda e: e.activation(out=NSq[j][:, g2 * 384:(g2 + 1) * 384], in_=p[:, 0:384], func=AF.Copy),
                                         reads=[pn], writes=[('N', j, g2)])
                                if j < 6:
                                    for g2 in range(2):
                                        p, pn = next_pz()
                                        for hh in range(3):
                                            h = g2 * 3 + hh
                                            n_, nn = Nj(j - 1, h)
                                            nt_, ntn = NTj(j - 1, h)
                                            P.op('pe', lambda e: e.matmul(p[:, hh * 128:(hh + 1) * 128], lhsT=n_, rhs=nt_, start=True, stop=True),
                                                 reads=[nn, ntn], writes=[pn])
                                        P.op('dve', lambda e: e.tensor_copy(out=NTS[nti(j)][:, g2 * 384:(g2 + 1) * 384], in_=p[:, 0:384]),
                                             reads=[pn], writes=[('NT', nti(j), g2)])
                            pT, pTn = next_pz()
                            for h in range(6):
                                hs = slice(h * 64, (h + 1) * 64)
                                P.op('pe', lambda e: e.matmul(pT[:, hs], lhsT=fm(h, 0), rhs=S16[:, hs], start=True, stop=False), reads=[('FM', h // 2), 'S16'], writes=[pTn])
                                P.op('pe', lambda e: e.matmul(pT[:, hs], lhsT=AT[:, h * 512 + 256:h * 512 + 384], rhs=V16[:, hs], start=False, stop=True),
                                     reads=[('AT', h), 'V16'], writes=[pTn])
                            P.op('dve', lambda e: e.tensor_copy(out=U32[:], in_=pT[:, 0:384]), reads=[pTn], writes=['U32'])
                            for j in range(7):
                                pU, pUn = next_pz()
                                for h in range(6):
                                    hs = slice(h * 64, (h + 1) * 64)
                                    n_, nn = Nj(j, h)
                                    P.op('pe', lambda e: e.matmul(pU[:, hs], lhsT=n_, rhs=U32[:, hs], start=True, stop=True), reads=[nn, 'U32'], writes=[pUn])
                                P.op('dve', lambda e: e.tensor_tensor(out=U32[:], in0=pU[:, 0:384], in1=U32[:], op=ALU.add), reads=[pUn, 'U32'], writes=['U32'])
                            P.op('act', lambda e: e.activation(out=U16[1][:], in_=U32[:], func=AF.Copy), reads=['U32'], writes=['U16_1'])
                            Uf = U16[1]
                            ufn = 'U16_1'
                            ypz = pz[4 + ci % 2]
                            ypn = 'pz%d' % (4 + ci % 2)
                            pD, pDn = next_pz()
                            for h in range(6):
                                hs = slice(h * 64, (h + 1) * 64)
                                P.op('pe', lambda e: e.matmul(ypz[:, hs], lhsT=fm(h, 1), rhs=S16[:, hs], start=True, stop=False), reads=[('FM', h // 2), 'S16'], writes=[ypn])
                                P.op('pe', lambda e: e.matmul(ypz[:, hs], lhsT=AT[:, h * 512 + 128:h * 512 + 256], rhs=Uf[:, hs], start=False, stop=False),
                                     reads=[('AT', h), ufn], writes=[ypn])
                                P.op('pe', lambda e: e.matmul(ypz[:, hs], lhsT=AT[:, h * 512 + 384:h * 512 + 512], rhs=V16[:, hs], start=False, stop=True),
                                     reads=[('AT', h), 'V16'], writes=[ypn])
                                P.op('pe', lambda e: e.matmul(pD[0:64, hs], lhsT=BH[:, hs], rhs=Uf[:, hs], start=True, stop=False), reads=['BH', ufn], writes=[pDn])
                                P.op('pe', lambda e: e.matmul(pD[0:64, hs], lhsT=KH[:, hs], rhs=V16[:, hs], start=False, stop=True), reads=['KH', 'V16'], writes=[pDn])
                            for h in range(6):
                                hs = slice(h * 64, (h + 1) * 64)
                                P.op('dve', lambda e: e.scalar_tensor_tensor(out=SF[:, hs], in0=SF[:, hs], scalar=wc[:, h:h + 1], in1=pD[0:64, hs],
                                                                             op0=ALU.mult, op1=ALU.add), reads=['SF', 'wc', pDn], writes=['SF'])
                            P.op('act', lambda e: e.activation(out=S16[:], in_=SF[:], func=AF.Copy), reads=['SF'], writes=['S16'])
                            if d == 0:
                                yb_ = ych[b]
                                P.op('act', lambda e: e.activation(out=yb_[:], in_=ypz[:, 0:384], func=AF.Copy), reads=[ypn], writes=['ych%d' % b])
                                P.dma('pool', YF[t0:t0 + 128, :], yb_[:], reads=['ych%d' % b], writes=[('YF', t0)])
                            else:
                                P.dma('sp', yft[:], YF[t0:t0 + 128, :], reads=[('YF', t0)], writes=['yft'])
                                P.dma('sp', gbc[:], GB[t0:t0 + 128, :], writes=['gbc'])
                                P.op('dve', lambda e: e.tensor_tensor(out=yc[:], in0=ypz[:, 0:384], in1=yft[:], op=ALU.add), reads=[ypn, 'yft'], writes=['yc'])
                                P.op('dve', lambda e: e.tensor_reduce(out=st6[:, 0:6], in_=yc[:].rearrange("p (h d) -> p h d", d=64), axis=AX.X, op=ALU.add),
                                     reads=['yc'], writes=['st6'])
                                P.op('dve', lambda e: e.tensor_scalar(out=st6[:, 0:6], in0=st6[:, 0:6], scalar1=-1.0 / 64, scalar2=None, op0=ALU.mult), reads=['st6'], writes=['st6'])
                                for h in range(6):
                                    P.op('dve', lambda e: e.tensor_scalar(out=yc[:, h * 64:(h + 1) * 64], in0=yc[:, h * 64:(h + 1) * 64], scalar1=st6[:, h:h + 1],
                                                                          scalar2=None, op0=ALU.add), reads=['yc', 'st6'], writes=['yc'])
                                P.op('pool', lambda e: e.tensor_tensor(out=ysq[:], in0=yc[:], in1=yc[:], op=ALU.mult), reads=['yc'], writes=['ysq'])
                                P.op('dve', lambda e: e.tensor_reduce(out=st6[:, 6:12], in_=ysq[:].rearrange("p (h d) -> p h d", d=64), axis=AX.X, op=ALU.add),
                                     reads=['ysq'], writes=['st6'])
                                P.op('act', lambda e: e.activation(out=st6[:, 6:12], in_=st6[:, 6:12], func=AF.Sqrt, scale=1.0 / 64, bias=epsl[:]),
                                     reads=['st6', 'epsl'], writes=['st6'])
                                P.op('dve', lambda e: e.reciprocal(out=st6[:, 6:12], in_=st6[:, 6:12]), reads=['st6'], writes=['st6'])
                                for h in range(6):
                                    P.op('dve', lambda e: e.scalar_tensor_tensor(out=yc[:, h * 64:(h + 1) * 64], in0=yc[:, h * 64:(h + 1) * 64], scalar=st6[:, 6 + h:7 + h],
                                                                                 in1=lnw[:, h * 64:(h + 1) * 64], op0=ALU.mult, op1=ALU.mult),
                                         reads=['yc', 'st6', 'lnw'], writes=['yc'])
                                P.op('pool', lambda e: e.tensor_tensor(out=yc[:], in0=yc[:], in1=lnb[:], op=ALU.add), reads=['yc', 'lnb'], writes=['yc'])
                                P.op('pool', lambda e: e.tensor_tensor(out=yc[:], in0=yc[:], in1=gbc[:, WC:2 * WC], op=ALU.add), reads=['yc', 'gbc'], writes=['yc'])
                                P.op('pool', lambda e: e.tensor_tensor(out=ocb[b][:], in0=yc[:], in1=gbc[:, 0:WC], op=ALU.mult), reads=['yc', 'gbc'], writes=['ocb%d' % b])
                                P.dma('pool', OS[t0:t0 + 128, 640:1024], ocb[b][:], reads=['ocb%d' % b], writes=[('OSC', t0)])
                    P.barrier()
                    if stop_after == 5 + d * 0.5:
                        return nc

            with ExitStack() as ph:
                def sbp(name, shape, dt=F32):
                    return ph.enter_context(nc.sbuf_tensor(name + '_L%d' % l, list(shape), dt))
                Wout = sbp("Wout", [128, 8, D], BF16)
                Wgu = sbp("Wgu", [128, 8, 2 * DFF], BF16)
                Wdn = sbp("Wdn", [128, 22, D], BF16)
                gffn = sbp("gffn", [128, D])
                gfin = sbp("gfin", [128, D]) if last else None
                xt = [sbp("p5x%d" % i, [128, D]) for i in range(2)]
                ot = [sbp("p5o%d" % i, [128, D], BF16) for i in range(2)]
                oT = sbp("p5oT", [128, D], BF16)
                h2 = sbp("p5h2", [128, D], BF16)
                h2T = sbp("p5h2T", [128, D], BF16)
                sg = [sbp("p5sg%d" % i, [128, 512]) for i in range(1)]
                gt = sbp("p5g", [128, DFF], BF16)
                gT = sbp("p5gT", [128, DFF], BF16)
                yo = [sbp("p5y0", [128, D]) if last else None] * 2
                load_weight(Wout, 'Wout', w_out[l], 8, D)
                load_weight(Wgu, 'Wgu', w_gate_up[l], 8, 2 * DFF)
                load_weight(Wdn, 'Wdn', w_down[l], 22, D)
                bcast_load(gffn[:], 'gffn', norm_ffn[l:l + 1, :])
                if last:
                    bcast_load(gfin[:], 'gfin', norm_final[0:1, :])
                for ti in range(T // 128):
                    g0 = ti * 128
                    b = ti % 2
                    xn = 'p5x%d' % b
                    on = 'p5o%d' % b
                    P.dma('sp', xt[b][:], Xsrc[g0:g0 + 128, :], reads=[('XR', g0)], writes=[xn])
                    P.dma('sp', ot[b][:], OS[g0:g0 + 128, :], reads=[('OS', g0)], writes=[on])
                    transpose_bf(ot[b], on, oT, 'p5oT', 8)
                    for c in range(2):
                        p, pn = next_pz()
                        for k in range(8):
                            P.op('pe', lambda e, p=p, k=k, c=c: e.matmul(p[:, :], lhsT=oT[:, k * 128:(k + 1) * 128],
                                                                          rhs=Wout[:, k, c * 512:(c + 1) * 512], start=(k == 0), stop=(k == 7)),
                                 reads=['p5oT', 'Wout'], writes=[pn])
                        P.op('dve', lambda e, p=p, c=c: e.tensor_tensor(out=xt[b][:, c * 512:(c + 1) * 512], in0=p[:, :],
                                                                        in1=xt[b][:, c * 512:(c + 1) * 512], op=ALU.add),
                             reads=[pn, xn], writes=[xn])
                    rmsnorm_tile(xt[b][:], xn, gffn[:], 'gffn', h2[:], 'p5h2')
                    transpose_bf(h2, 'p5h2', h2T, 'p5h2T', 8)
                    for i in range(6):
                        w = 512 if i < 5 else 256
                        pg, pgn = next_pz()
                        for k in range(8):
                            P.op('pe', lambda e, pg=pg, k=k, i=i, w=w: e.matmul(pg[:, 0:w], lhsT=h2T[:, k * 128:(k + 1) * 128],
                                                                                 rhs=Wgu[:, k, i * 512:i * 512 + w], start=(k == 0), stop=(k == 7)),
                                 reads=['p5h2T', 'Wgu'], writes=[pgn])
                        pu, pun = next_pz()
                        for k in range(8):
                            P.op('pe', lambda e, pu=pu, k=k, i=i, w=w: e.matmul(pu[:, 0:w], lhsT=h2T[:, k * 128:(k + 1) * 128],
                                                                                 rhs=Wgu[:, k, DFF + i * 512:DFF + i * 512 + w], start=(k == 0), stop=(k == 7)),
                                 reads=['p5h2T', 'Wgu'], writes=[pun])
                        sj = 0
                        P.op('act', lambda e, pg=pg, w=w, sj=sj: e.activation(out=sg[sj][:, 0:w], in_=pg[:, 0:w], func=AF.Silu),
                             reads=[pgn], writes=['p5sg%d' % sj])
                        P.op('dve', lambda e, pu=pu, w=w, sj=sj, i=i: e.tensor_tensor(out=gt[:, i * 512:i * 512 + w], in0=pu[:, 0:w],
                                                                                    in1=sg[sj][:, 0:w], op=ALU.mult),
                             reads=[pun, 'p5sg%d' % sj], writes=['p5g'])
                    transpose_bf(gt, 'p5g', gT, 'p5gT', 22)
                    for c in range(2):
                        p, pn = next_pz()
                        for k in range(22):
                            P.op('pe', lambda e, p=p, k=k, c=c: e.matmul(p[:, :], lhsT=gT[:, k * 128:(k + 1) * 128],
                                                                          rhs=Wdn[:, k, c * 512:(c + 1) * 512], start=(k == 0), stop=(k == 21)),
                                 reads=['p5gT', 'Wdn'], writes=[pn])
                        P.op('dve', lambda e, p=p, c=c: e.tensor_tensor(out=xt[b][:, c * 512:(c + 1) * 512], in0=p[:, :],
                                                                        in1=xt[b][:, c * 512:(c + 1) * 512], op=ALU.add),
                             reads=[pn, xn], writes=[xn])
                    if last:
                        rmsnorm_tile(xt[b][:], xn, gfin[:], 'gfin', yo[b][:], 'p5y0')
                        P.dma('pool', y_out[g0:g0 + 128, :], yo[b][:], reads=['p5y0'], writes=[('Y', g0)])
                    else:
                        P.dma('pool', XR[g0:g0 + 128, :], xt[b][:], reads=[xn], writes=[('XR', g0)])
                P.barrier()
                if stop_after == 5:
                    return nc
        P.barrier()
    return nc


def mixers(nc, P, env):
    pass


def host_consts(smax):
    pos = np.arange(smax, dtype=np.float32)[:, None]
    tabs = []
    for rot, rep in ((16, 12), (8, 16)):
        half = rot // 2
        inv = (ROPE_THETA ** (-(np.arange(half, dtype=np.float32) * 2.0 / rot))).astype(np.float32)
        ang = (pos * inv[None, :]).astype(np.float32)
        tabs.append((np.tile(np.cos(ang), (1, rep)), np.tile(np.sin(ang), (1, rep))))
    cA, sA = tabs[0][0][:, 0:48], tabs[0][1][:, 0:48]
    tab = np.concatenate([cA, cA * 0.125, sA, sA * 0.125, tabs[1][0], tabs[1][1], np.zeros((smax, 96), np.float32)], axis=1).astype(np.float32)
    assert tab.shape[1] == NTAB
    kk = np.arange(128)[:, None, None]
    m = np.arange(20)[None, :, None]
    qq = np.arange(512)[None, None, :]
    dl = (m - 8) * 128 + kk - qq
    c = (np.abs(dl) <= 64).astype(np.float32) + ((dl % 4 == 0) & (np.abs(dl) <= 256)) + ((dl % 16 == 0) & (np.abs(dl) <= 1024))
    lnc = np.where(c > 0, np.log(np.maximum(c, 1.0)), -30000.0).astype(np.float32)
    maskA = lnc.reshape(128, 20 * 512).astype(ml_dtypes.bfloat16)
    ident = np.eye(128, dtype=np.float32)
    s = np.arange(128)[:, None]
    t = np.arange(128)[None, :]
    tri = np.concatenate([(s <= t), (s > t), (s >= t), (s < t)], axis=1).astype(np.float32)
    lt_, le_, gt_, ge_ = (s < t), (s <= t), (s > t), (s >= t)
    mask_c = np.concatenate([lt_, le_, lt_, le_, gt_, gt_, gt_, gt_, ge_, gt_, ge_, lt_, lt_, lt_], axis=1).astype(np.float32)
    assert mask_c.shape == (128, 1792)
    return dict(rope_tab=tab, maskA=maskA, ident_f=ident, tri_c=tri, mask_c=mask_c)


SEQ_LENS_FULL = [2048, 2048, 8192]


def make_in_maps(inputs, seq_lens, xs_per_core):
    smax = max(seq_lens)
    consts = host_consts(smax)
    shared = {}
    f = lambda a: np.ascontiguousarray(np.asarray(a, dtype=np.float32))
    shared["norm_mix"] = f(inputs["norm_mix"])
    shared["w_in"] = f(inputs["w_in"])
    shared["w_out"] = f(inputs["w_out"])
    shared["lam_qk"] = f(inputs["lam_qk"]).reshape(DEPTH, 128)
    shared["subln_w"] = f(inputs["subln_w"])
    shared["mu_c"] = f(inputs["mu_c"])
    shared["w0"] = f(inputs["w0"]).reshape(DEPTH, 2 * WC)
    shared["w_up"] = f(inputs["w_up"])
    shared["a0"] = f(inputs["a0"]).reshape(DEPTH, 2 * WC)
    shared["a_up"] = f(inputs["a_up"])
    shared["g_up"] = f(inputs["g_up"])
    shared["v0"] = f(inputs["v0"])
    shared["v_down"] = f(inputs["v_down"])
    shared["v_up"] = f(inputs["v_up"])
    shared["k_k"] = f(inputs["k_k"])
    shared["k_a"] = f(inputs["k_a"])
    shared["r_k"] = f(inputs["r_k"]).reshape(DEPTH, WC)
    shared["lnx_w"] = f(inputs["lnx_w"])
    shared["lnx_b"] = f(inputs["lnx_b"])
    shared["norm_ffn"] = f(inputs["norm_ffn"])
    shared["w_gate_up"] = f(inputs["w_gate_up"])
    shared["w_down"] = f(inputs["w_down"])
    shared["norm_final"] = f(inputs["norm_final"]).reshape(1, D)
    shared.update(consts)
    maps = []
    for xs in xs_per_core:
        m = dict(shared)
        m["x"] = np.ascontiguousarray(np.concatenate([np.asarray(a, np.float32).reshape(-1, D) for a in xs], axis=0))
        maps.append(m)
    return maps


def kernel(**inputs):
    xp = np.asarray(inputs["x_prompt"], np.float32)
    xs = np.asarray(inputs["x_sample"], np.float32)
    ncores = 8
    per_core = [[xp[2 * c], xp[2 * c + 1], xs[c]] for c in range(ncores)]
    nc = build(SEQ_LENS_FULL, DEPTH)
    in_maps = make_in_maps(inputs, SEQ_LENS_FULL, per_core)
    res = run_bass_kernel_spmd(nc, in_maps, core_ids=list(range(ncores)))
    yp = np.empty_like(xp)
    ys = np.empty_like(xs)
    for c in range(ncores):
        y = res.results[c]["y"]
        yp[2 * c] = y[0:2048]
        yp[2 * c + 1] = y[2048:4096]
        ys[c] = y[4096:12288]
    return (yp, ys)
```
